# Optimizing a Trainium2 kernel written in Bass

```python
import jax, jax.numpy as jnp
from jax import lax
import numpy as np

D_MODEL = 1024
BATCH = 8
SEQ = 4096
DEPTH = 2
DEC_BATCH = 1
DEC_SEQ = 16384
PAST_LEN = 128

GRID_W = 64
D_MIX = D_MODEL
HEAD_DIM = 64
NA_HEADS = 4
NA_ROWS = 8
NA_COLS = 16
NA_WIDTH = NA_HEADS * HEAD_DIM
FN_GROUPS = 4
FN_GROUP_DIM = 64
FN_WIDTH = FN_GROUPS * FN_GROUP_DIM
MLA_HEADS = 4
MLA_Q_LORA = 256
MLA_KV_LORA = 128
MLA_NOPE = 64
MLA_ROPE = 32
MLA_V = 64
MLA_WIDTH = MLA_HEADS * MLA_V
MLA_Q_BLOCK = 128
ROPE_THETA = 10000.0
SGU_HEADS = 4
SGU_HEAD_DIM = 64
SGU_WIDTH = SGU_HEADS * SGU_HEAD_DIM
SGU_CHUNK = 128
D_IN = 3 * NA_WIDTH + FN_WIDTH + MLA_Q_LORA + MLA_KV_LORA + MLA_ROPE + 2 * SGU_WIDTH
D_FF = 2816
CONV_WIDTH = 3
ALPHA = (2 * DEPTH) ** 0.25
BETA = (8 * DEPTH) ** -0.25
LN_EPS = 1e-5
RMS_EPS = 1e-6

kernel_name = 'hymba_natten_fnet_mla_gmlp_encoder'


def _layer_norm(x, g, b):
    xf = x.astype(jnp.float32)
    mu = jnp.mean(xf, axis=-1, keepdims=True)
    var = jnp.mean(jnp.square(xf - mu), axis=-1, keepdims=True)
    return ((xf - mu) * lax.rsqrt(var + LN_EPS) * g + b).astype(x.dtype)


def _rms_norm(x, g):
    xf = x.astype(jnp.float32)
    ms = jnp.mean(jnp.square(xf), axis=-1, keepdims=True)
    return (xf * lax.rsqrt(ms + RMS_EPS) * g).astype(x.dtype)


def _rope_tables(T):
    inv_freq = ROPE_THETA ** (-jnp.arange(0, MLA_ROPE, 2, dtype=jnp.float32) / MLA_ROPE)
    ang = jnp.arange(T, dtype=jnp.float32)[:, None] * inv_freq[None, :]
    return jnp.cos(ang), jnp.sin(ang)


def _rope(x, cos, sin):
    x1, x2 = jnp.split(x, 2, axis=-1)
    return jnp.concatenate([x1 * cos - x2 * sin, x1 * sin + x2 * cos], axis=-1).astype(x.dtype)


def _natten(q, k, v, rpb):
    B, T, H, Dh = q.shape
    rows = T // GRID_W
    kh = min(NA_ROWS, rows)
    qg = q.reshape(B, rows, GRID_W, H, Dh)
    kg = k.reshape(B, rows, GRID_W, H, Dh)
    vg = v.reshape(B, rows, GRID_W, H, Dh)
    col = jnp.arange(GRID_W)
    col_start = jnp.clip(col - NA_COLS // 2, 0, GRID_W - NA_COLS)
    col_idx = col_start[:, None] + jnp.arange(NA_COLS)[None, :]
    col_off = col_idx - col[:, None] + (NA_COLS - 1)
    scale = Dh ** -0.5

    def row_block(r):
        r_start = jnp.clip(r - kh // 2, 0, rows - kh)
        q_r = lax.dynamic_index_in_dim(qg, r, axis=1, keepdims=False)
        k_win = lax.dynamic_slice_in_dim(kg, r_start, kh, axis=1)[:, :, col_idx]
        v_win = lax.dynamic_slice_in_dim(vg, r_start, kh, axis=1)[:, :, col_idx]
        row_off = r_start + jnp.arange(kh) - r + (NA_ROWS - 1)
        bias = rpb[:, row_off][:, :, col_off]
        s = (jnp.einsum('bqhd,biqjhd->bhqij', q_r, k_win).astype(jnp.float32) * scale
             + jnp.transpose(bias, (0, 2, 1, 3)).astype(jnp.float32))
        p = jax.nn.softmax(s.reshape(B, H, GRID_W, kh * NA_COLS), axis=-1).reshape(s.shape).astype(v.dtype)
        return jnp.einsum('bhqij,biqjhd->bqhd', p, v_win)

    o = lax.map(row_block, jnp.arange(rows))
    return jnp.moveaxis(o, 0, 1).reshape(B, T, H * Dh)


def _fourier_mix(f):
    B, T, _ = f.shape
    fg = f.reshape(B, T, FN_GROUPS, FN_GROUP_DIM).astype(jnp.float32)
    out = jnp.real(jnp.fft.fft2(fg, axes=(1, 3), norm='ortho'))
    return out.reshape(B, T, FN_WIDTH).astype(f.dtype)


def _mla(cq, ckv, kr, q_g, w_uq, kv_g, w_ukv, cos, sin):
    B, T, _ = cq.shape
    q = jnp.einsum('btr,rhe->bthe', _rms_norm(cq, q_g), w_uq)
    q_nope = q[..., :MLA_NOPE]
    q_rope = _rope(q[..., MLA_NOPE:], cos[:, None, :], sin[:, None, :])
    kv = jnp.einsum('btr,rhe->bthe', _rms_norm(ckv, kv_g), w_ukv)
    k_nope = kv[..., :MLA_NOPE]
    v = kv[..., MLA_NOPE:]
    k_rope = _rope(kr, cos, sin)
    nb = T // MLA_Q_BLOCK
    scale = (MLA_NOPE + MLA_ROPE) ** -0.5

    def to_blocks(a):
        return jnp.moveaxis(a.reshape((B, nb, MLA_Q_BLOCK) + a.shape[2:]), 1, 0)

    def block(args):
        qn, qr = args
        s = jnp.einsum('bqhe,bkhe->bhqk', qn, k_nope) + jnp.einsum('bqhe,bke->bhqk', qr, k_rope)
        p = jax.nn.softmax(s.astype(jnp.float32) * scale, axis=-1).astype(v.dtype)
        return jnp.einsum('bhqk,bkhe->bqhe', p, v)

    o = lax.map(block, (to_blocks(q_nope), to_blocks(q_rope)))
    return jnp.moveaxis(o, 0, 1).reshape(B, T, MLA_WIDTH)


def _spatial_gating(z, ln_g, ln_b, w_s, b_s):
    B, T, _ = z.shape
    u, v = jnp.split(jax.nn.gelu(z), 2, axis=-1)
    v = v.reshape(B, T // SGU_CHUNK, SGU_CHUNK, SGU_HEADS, SGU_HEAD_DIM)
    v = _layer_norm(v, ln_g.reshape(SGU_HEADS, SGU_HEAD_DIM), ln_b.reshape(SGU_HEADS, SGU_HEAD_DIM))
    v = jnp.einsum('gpq,bcqgd->bcpgd', w_s, v) + b_s.T[:, :, None]
    return u * v.reshape(B, T, SGU_WIDTH)


def _mixer(h, cos, sin, w_in, na_rpb, mla_q_g, w_uq, mla_kv_g, w_ukv,
           sgu_ln_g, sgu_ln_b, sgu_w, sgu_b, w_out):
    B, T, _ = h.shape
    z = h @ w_in
    sizes = [NA_WIDTH, NA_WIDTH, NA_WIDTH, FN_WIDTH, MLA_Q_LORA, MLA_KV_LORA, MLA_ROPE, 2 * SGU_WIDTH]
    idx = [int(i) for i in np.cumsum(sizes)[:-1]]
    qa, ka, va, fb, cq, ckv, kr, sg = jnp.split(z, idx, axis=-1)
    hs = (B, T, NA_HEADS, HEAD_DIM)
    o_a = _natten(qa.reshape(hs), ka.reshape(hs), va.reshape(hs), na_rpb)
    o_b = _fourier_mix(fb)
    o_c = _mla(cq, ckv, kr, mla_q_g, w_uq, mla_kv_g, w_ukv, cos, sin)
    o_d = _spatial_gating(sg, sgu_ln_g, sgu_ln_b, sgu_w, sgu_b)
    mix = jnp.concatenate([o_a, o_b, o_c, o_d], axis=-1)
    return mix @ w_out


def _conv_ffn(x, w_up, conv_w, conv_b, w_down):
    h = x @ w_up
    hp = jnp.pad(h, ((0, 0), (CONV_WIDTH // 2, CONV_WIDTH // 2), (0, 0)))
    h = hp[:, :-2] * conv_w[0] + hp[:, 1:-1] * conv_w[1] + hp[:, 2:] * conv_w[2] + conv_b
    gate, val = jnp.split(h, 2, axis=-1)
    return (jax.nn.gelu(gate) * val) @ w_down


def _trunk(x, emb_ln_g, emb_ln_b, w_in, na_rpb, mla_q_g, w_uq, mla_kv_g, w_ukv,
           sgu_ln_g, sgu_ln_b, sgu_w, sgu_b, w_out, ln1_g, ln1_b,
           w_up, conv_w, conv_b, w_down, ln2_g, ln2_b):
    T = x.shape[1]
    cos, sin = _rope_tables(T)
    x = _layer_norm(x, emb_ln_g, emb_ln_b)
    for l in range(DEPTH):
        mix = _mixer(x, cos, sin, w_in[l], na_rpb[l], mla_q_g[l], w_uq[l], mla_kv_g[l], w_ukv[l],
                     sgu_ln_g[l], sgu_ln_b[l], sgu_w[l], sgu_b[l], w_out[l])
        x = _layer_norm(ALPHA * x + mix, ln1_g[l], ln1_b[l])
        x = _layer_norm(ALPHA * x + _conv_ffn(x, w_up[l], conv_w[l], conv_b[l], w_down[l]), ln2_g[l], ln2_b[l])
    return x


def setup_inputs(seed: int = 0) -> dict:
    key = jax.random.key(seed)
    ks = jax.random.split(key, 23)

    def nrm(k, shape, scale):
        return jax.random.normal(k, shape, jnp.float32) * scale

    L = DEPTH
    return {
        'x_prompt': nrm(ks[0], (BATCH, SEQ, D_MODEL), 1.0),
        'x_sample': nrm(ks[1], (DEC_BATCH, DEC_SEQ, D_MODEL), 1.0),
        'emb_ln_g': 1.0 + nrm(ks[2], (D_MODEL,), 0.05),
        'emb_ln_b': nrm(ks[3], (D_MODEL,), 0.02),
        'w_in': nrm(ks[4], (L, D_MODEL, D_IN), D_MODEL ** -0.5),
        'na_rpb': nrm(ks[5], (L, NA_HEADS, 2 * NA_ROWS - 1, 2 * NA_COLS - 1), 0.1),
        'mla_q_g': 1.0 + nrm(ks[6], (L, MLA_Q_LORA), 0.05),
        'w_uq': nrm(ks[7], (L, MLA_Q_LORA, MLA_HEADS, MLA_NOPE + MLA_ROPE), MLA_Q_LORA ** -0.5),
        'mla_kv_g': 1.0 + nrm(ks[8], (L, MLA_KV_LORA), 0.05),
        'w_ukv': nrm(ks[9], (L, MLA_KV_LORA, MLA_HEADS, MLA_NOPE + MLA_V), MLA_KV_LORA ** -0.5),
        'sgu_ln_g': 1.0 + nrm(ks[10], (L, SGU_WIDTH), 0.05),
        'sgu_ln_b': nrm(ks[11], (L, SGU_WIDTH), 0.02),
        'sgu_w': nrm(ks[12], (L, SGU_HEADS, SGU_CHUNK, SGU_CHUNK), SGU_CHUNK ** -0.5),
        'sgu_b': 1.0 + nrm(ks[13], (L, SGU_HEADS, SGU_CHUNK), 0.05),
        'w_out': nrm(ks[14], (L, D_MIX, D_MODEL), BETA * D_MIX ** -0.5),
        'ln1_g': 1.0 + nrm(ks[15], (L, D_MODEL), 0.05),
        'ln1_b': nrm(ks[16], (L, D_MODEL), 0.02),
        'w_up': nrm(ks[17], (L, D_MODEL, 2 * D_FF), D_MODEL ** -0.5),
        'conv_w': nrm(ks[18], (L, CONV_WIDTH, 2 * D_FF), 0.5),
        'conv_b': nrm(ks[19], (L, 2 * D_FF), 0.02),
        'w_down': nrm(ks[20], (L, D_FF, D_MODEL), BETA * D_FF ** -0.5),
        'ln2_g': 1.0 + nrm(ks[21], (L, D_MODEL), 0.05),
        'ln2_b': nrm(ks[22], (L, D_MODEL), 0.02),
    }


def reference(x_prompt, x_sample, emb_ln_g, emb_ln_b, w_in, na_rpb, mla_q_g, w_uq, mla_kv_g, w_ukv,
              sgu_ln_g, sgu_ln_b, sgu_w, sgu_b, w_out, ln1_g, ln1_b,
              w_up, conv_w, conv_b, w_down, ln2_g, ln2_b):
    y_prompt = _trunk(x_prompt, emb_ln_g, emb_ln_b, w_in, na_rpb, mla_q_g, w_uq, mla_kv_g, w_ukv,
                      sgu_ln_g, sgu_ln_b, sgu_w, sgu_b, w_out, ln1_g, ln1_b,
                      w_up, conv_w, conv_b, w_down, ln2_g, ln2_b)
    y_sample = _trunk(x_sample, emb_ln_g, emb_ln_b, w_in, na_rpb, mla_q_g, w_uq, mla_kv_g, w_ukv,
                      sgu_ln_g, sgu_ln_b, sgu_w, sgu_b, w_out, ln1_g, ln1_b,
                      w_up, conv_w, conv_b, w_down, ln2_g, ln2_b)
    return (y_prompt, y_sample)
```

```python
import math
from contextlib import ExitStack
import numpy as np
import ml_dtypes
import concourse.bass as bass
import concourse.mybir as mybir
from concourse.bass_utils import run_bass_kernel_spmd

F32 = mybir.dt.float32
BF = mybir.dt.bfloat16
AF = mybir.ActivationFunctionType
ALU = mybir.AluOpType
NPBF = ml_dtypes.bfloat16

D = 1024
DEPTH = 2
NCORE = 8
DFF = 2816
ZC = 1984
ALPHA = (2 * DEPTH) ** 0.25
LN_EPS = 1e-5
RMS_EPS = 1e-6
O_KAT = 0
O_VA = 32768
O_Y = 65536
O_KMT = 131072
O_VM = 180224
CHW = 180224 + 128 * 272
C_LN1G, C_LN1B, C_LN2G, C_LN2B = 0, 8, 16, 24
C_QG, C_KVG = 32, 34
C_CW0, C_CW1, C_CW2, C_CB = 35, 79, 123, 167
C_EG, C_EB = 211, 219
NCOL = 227


class Res:
    __slots__ = ("name", "w", "r", "sem", "nd")

    def __init__(s, name, sem=None):
        s.name = name; s.w = {}; s.r = {}; s.sem = sem; s.nd = 0


class KB:
    def __init__(s, nc):
        s.nc = nc
        s.eng = {"pe": nc.tensor, "act": nc.scalar, "dve": nc.vector, "pool": nc.gpsimd, "sp": nc.sync}
        s.sem = {n: nc.alloc_semaphore("e_" + n) for n in s.eng}
        s.cnt = {n: 0 for n in s.eng}
        s.waited = {}
        s.dres = []
        s.bcount = 0
        s.bsem = nc.alloc_semaphore("bar")
        s.rmap = {}
        s.ninst = 0

    def res(s, name, dma=False):
        if name in s.rmap:
            return s.rmap[name]
        r = Res(name, s.nc.alloc_semaphore("r_" + name) if dma else None)
        if dma:
            s.dres.append(r)
        s.rmap[name] = r
        return r

    def _wait(s, e, sem, val):
        key = (e, sem.num)
        if s.waited.get(key, 0) >= val:
            return
        s.eng[e].wait_ge(sem, val)
        s.waited[key] = val

    def _deps(s, e, reads, writes, skipw=()):
        d = {}
        for r in reads:
            for k, v in r.w.items():
                if k not in d or d[k][1] < v[1]:
                    d[k] = v
        for w in writes:
            for src in ((w.r,) if w in skipw else (w.w, w.r)):
                for k, v in src.items():
                    if k not in d or d[k][1] < v[1]:
                        d[k] = v
        for k, (sem, val) in d.items():
            if e == "pe" and sem is s.sem["pe"]:
                continue
            s._wait(e, sem, val)

    def op(s, e, fn, reads=(), writes=(), inc=True):
        s._deps(e, reads, writes)
        ins = fn(s.eng[e])
        s.ninst += 1
        val = s.cnt[e] + 1
        sem = s.sem[e]
        if inc:
            ins.then_inc(sem, 1)
            s.cnt[e] = val
        ev = (sem, val)
        for r in reads:
            r.r[sem.num] = ev
        for w in writes:
            w.w[sem.num] = ev
            w.r = {}
        return ins

    def dma(s, q, out, in_, src, dst, waw=True, slow=False):
        s._deps(q, (src,), (dst,), skipw=() if waw else (dst,))
        if slow:
            ins = s.eng[q].dma_start(out=out, in_=in_, allow_slow_non_contiguous=True)
        else:
            ins = s.eng[q].dma_start(out=out, in_=in_)
        s.ninst += 1
        sr = dst
        if q == "pool":
            if src.sem is None:
                src.sem = s.nc.alloc_semaphore("s_" + src.name)
                s.dres.append(src)
            sr = src
        sr.nd += 1
        ins.then_inc(sr.sem, 16)
        ev = (sr.sem, 16 * sr.nd)
        src.r[sr.sem.num] = ev
        dst.w[sr.sem.num] = ev
        dst.r = {}

    def barrier(s):
        e0 = "sp"
        for f in s.eng:
            if f != e0 and s.cnt[f] > 0:
                s._wait(e0, s.sem[f], s.cnt[f])
        for r in s.dres:
            if r.nd > 0:
                s._wait(e0, r.sem, 16 * r.nd)
        s.bcount += 1
        s.eng[e0].sem_inc(s.bsem, 1)
        for e in s.eng:
            if e != e0:
                s.eng[e].wait_ge(s.bsem, s.bcount)
        for e in s.eng:
            for f in s.eng:
                k = (e, s.sem[f].num)
                s.waited[k] = max(s.waited.get(k, 0), s.cnt[f])
            for r in s.dres:
                k = (e, r.sem.num)
                s.waited[k] = max(s.waited.get(k, 0), 16 * r.nd)


class Job:
    pass


def _mk_jobs(TP, TS):
    P = Job(); P.name = "p"; P.T = TP; P.L = TP; P.dyn = False
    S = Job(); S.name = "s"; S.T = TS; S.L = TS // NCORE; S.dyn = True
    for J in (P, S):
        J.NCH = J.L // 128
        J.TCH = J.T // 128
        J.N1 = J.T // 128
        J.NK2 = J.L // J.N1
        J.NB = J.L // 512
    return P, S


def build(TP, TS):
    nc = bass.Bass("TRN2", target_bir_lowering=False)
    kb = KB(nc)
    P, S = _mk_jobs(TP, TS)
    jobs = (P, S)

    def din(name, shape, dt=F32):
        return nc.dram_tensor(name, list(shape), dt, kind="ExternalInput").ap()

    def dscr(name, shape, dt):
        return nc.dram_tensor(name, list(shape), dt).ap()

    for J in jobs:
        J.x = din("x_" + J.name, [J.L, D])
        J.y = nc.dram_tensor("y_" + J.name, [J.L, D], F32, kind="ExternalOutput").ap()
        J.bt = din("bt_" + J.name, [DEPTH, 35, 128, 512])
        J.rope = din("rope_" + J.name, [2, 32, J.L])
        J.f1 = din("f1_" + J.name, [2, J.N1, 2 * J.N1], BF)
        J.g3 = din("g3_" + J.name, [2, 128, J.N1 * J.NK2], BF)
    win = din("win", [DEPTH, D, ZC])
    wuq = din("wuq", [DEPTH, 256, 512])
    wukv = din("wukv", [DEPTH, 128, 512])
    swt = din("swt", [DEPTH, 128, 512])
    wout = din("wout", [DEPTH, D, D])
    wup = din("wup", [DEPTH, D, 2 * DFF])
    wdn = din("wdn", [DEPTH, DFF, D])
    cols = din("cols", [DEPTH, 128, NCOL])
    sgb = din("sgb", [DEPTH, 3, 512])
    cst = din("cst", [128, 2, 512], BF)
    ident_in = din("ident", [128, 128])
    R_in = kb.res("inputs")

    for J in jobs:
        J.XT = dscr("XT_" + J.name, [8, 128, J.L], F32)
        J.X1T = dscr("X1T_" + J.name, [8, 128, J.L], F32)
        J.MIXT = dscr("MIXT_" + J.name, [D, J.L], BF)
        J.QAT = dscr("QAT_" + J.name, [256, J.L], BF)
        J.QMT = dscr("QMT_" + J.name, [4, 96, J.L], BF)
        J.SHA = dscr("SHA_" + J.name, [J.TCH + 6, CHW], BF)
        J.rXT = kb.res("XT" + J.name, True); J.rX1T = kb.res("X1T" + J.name, True)
        J.rMIXT = kb.res("MIXT" + J.name, True); J.rQAT = kb.res("QAT" + J.name, True)
        J.rQMT = kb.res("QMT" + J.name, True); J.rSHA = kb.res("SHA" + J.name, True)
    SHL = dscr("SHL", [S.NCH, CHW], BF); rSHL = kb.res("SHL", True)
    LWIN = dscr("LWIN", [S.NCH + 6, 65536], BF); rLWIN = kb.res("LWIN", True)
    HL = dscr("HL", [2, D], BF); rHL = kb.res("HL", True)
    HGP = dscr("HGP", [18, D], BF); rHGP = kb.res("HGP", True)
    WINb = dscr("WINb", [DEPTH, 128, 8, ZC], BF)
    WUQb = dscr("WUQb", [DEPTH, 128, 2, 512], BF)
    WUKVb = dscr("WUKVb", [DEPTH, 128, 512], BF)
    SWTb = dscr("SWTb", [DEPTH, 128, 512], BF)
    WOUTb = dscr("WOUTb", [DEPTH, 128, 8, D], BF)
    WUPb = dscr("WUPb", [DEPTH, 128, 8, 2 * DFF], BF)
    WDNb = dscr("WDNb", [DEPTH, 128, 22, D], BF)
    rWb = kb.res("Wb", True)
    for J in jobs:
        J.BTb = dscr("BTb_" + J.name, [DEPTH, 128, 35, 512], BF)

    gs = ExitStack()

    uid = [0]

    def sb(es, name, shape, dt):
        uid[0] += 1
        return es.enter_context(nc.sbuf_tensor("%s_%d" % (name, uid[0]), list(shape), dt))

    PS = [gs.enter_context(nc.psum_tensor("ps%d" % i, [128, 512], F32)) for i in range(8)]
    rPS = [kb.res("ps%d" % i) for i in range(8)]
    identf = sb(gs, "identf", [128, 128], F32)
    identb = sb(gs, "identb", [128, 128], BF)
    onesb = sb(gs, "onesb", [128, 128], BF)
    onesf = sb(gs, "onesf", [128, 128], F32)
    ones_mean = sb(gs, "ones_mean", [128, 128], BF)
    zerob = sb(gs, "zerob", [128, 2048], BF)
    colsb = sb(gs, "colsb", [128, DEPTH, NCOL], F32)
    epst = sb(gs, "epst", [128, 2], F32)
    rC = kb.res("consts", True)
    kb.op("dve", lambda e: e.memset(epst[:, 0:1], LN_EPS), writes=(rC,))
    kb.op("dve", lambda e: e.memset(epst[:, 1:2], RMS_EPS), writes=(rC,))
    kb.op("dve", lambda e: e.memset(onesf[:], 1.0), writes=(rC,))
    kb.op("dve", lambda e: e.memset(onesb[:], 1.0), writes=(rC,))
    kb.op("dve", lambda e: e.memset(ones_mean[:], 1.0 / D), writes=(rC,))
    kb.op("dve", lambda e: e.memset(zerob[:], 0.0), writes=(rC,))
    kb.dma("sp", identf[:], ident_in, R_in, rC)
    kb.op("dve", lambda e: e.tensor_copy(out=identb[:], in_=identf[:]), reads=(rC,), writes=(rC,))
    for l in range(DEPTH):
        kb.dma("sp", colsb[:, l, :], cols[l], R_in, rC)
    pid = None

    pidc = {}

    def getpid(q):
        if q not in pidc:
            pidc[q] = kb.eng[q].partition_id()
        return pidc[q]

    with ExitStack() as es:
        stg = [sb(es, "wstg%d" % i, [128, 2048], F32) for i in range(3)]
        stb = [sb(es, "wstb%d" % i, [128, 2048], BF) for i in range(3)]
        rstg = [kb.res("wstg%d" % i, True) for i in range(3)]
        rstb = [kb.res("wstb%d" % i) for i in range(3)]
        pieces = []

        def add_w(src2d, dst2d, ncols):
            for c0 in range(0, ncols, 2048):
                n = min(2048, ncols - c0)
                pieces.append((src2d[:, c0:c0 + n], dst2d[:, c0:c0 + n], n))
        for l in range(DEPTH):
            for k in range(8):
                add_w(win[l, k * 128:(k + 1) * 128, :], WINb[l, :, k, :], ZC)
                add_w(wout[l, k * 128:(k + 1) * 128, :], WOUTb[l, :, k, :], D)
                for half in range(2):
                    for f0, nf in ((0, 16), (16, 6)):
                        pieces.append((wup[l, k * 128:(k + 1) * 128, half * DFF + f0 * 128:half * DFF + (f0 + nf) * 128],
                                       WUPb[l, :, k, :].rearrange("p (f h j) -> p h f j", h=2, j=128)[:, half, f0:f0 + nf, :], nf * 128))
            for k in range(22):
                add_w(wdn[l, k * 128:(k + 1) * 128, :], WDNb[l, :, k, :], D)
            for k in range(2):
                add_w(wuq[l, k * 128:(k + 1) * 128, :], WUQb[l, :, k, :], 512)
            add_w(wukv[l], WUKVb[l], 512)
            add_w(swt[l], SWTb[l], 512)
            for J in jobs:
                for v in range(35):
                    add_w(J.bt[l, v], J.BTb[l, :, v, :], 512)
        cast_eng = ["dve", "act", "pool"]
        for i, (src, dst, n) in enumerate(pieces):
            b = i % 3
            kb.dma("sp", stg[b][:, :n], src, R_in, rstg[b])
            ce = cast_eng[i % 3]
            if ce == "act":
                kb.op("act", lambda e, b=b, n=n: e.copy(out=stb[b][:, :n], in_=stg[b][:, :n]), reads=(rstg[b],), writes=(rstb[b],))
            else:
                kb.op(ce, lambda e, b=b, n=n: e.tensor_copy(out=stb[b][:, :n], in_=stg[b][:, :n]), reads=(rstg[b],), writes=(rstb[b],))
            srcb = stb[b][:, :n]
            if len(dst.shape) == 3:
                srcb = srcb.rearrange("p (f j) -> p f j", j=128)
            kb.dma("pool", dst, srcb, rstb[b], rWb, waw=False)
        for J in jobs:
            for c in list(range(3)) + list(range(J.TCH + 3, J.TCH + 6)):
                kb.dma("pool", J.SHA[c].rearrange("(p f) -> p f", p=128), zerob[:, :CHW // 128], rC, J.rSHA, waw=False)
        kb.dma("pool", HGP[0:1, :], zerob[0:1, :D], rC, rHGP, waw=False)
        kb.dma("pool", HGP[17:18, :], zerob[0:1, :D], rC, rHGP, waw=False)
        kb.barrier()

    def layer_norm(es_tmp, xin, rxin, n, gcol, bcol, l, yf, ryf, yb, ryb, psA, psB, tag):
        xb = es_tmp["xb"]; xsq = es_tmp["xsq"]; msb = es_tmp["msb"]; rs = es_tmp["rs"]; tmp = xin
        rt = dict(es_tmp["r"]); rt["tmp"] = rxin
        kb.op("act", lambda e: e.copy(out=xb[:, :, :n], in_=xin[:, :, :n]), reads=(rxin,), writes=(rt["xb"],))
        kb.op("act", lambda e: e.activation(out=xsq[:, :, :n], in_=xin[:, :, :n], func=AF.Square), reads=(rxin,), writes=(rt["xsq"],))
        for c in range(8):
            kb.op("pe", lambda e, c=c: e.matmul(PS[psA][:, :n], lhsT=ones_mean[:], rhs=xb[:, c, :n], start=(c == 0), stop=(c == 7)),
                  reads=(rt["xb"], rC), writes=(rPS[psA],), inc=(c == 7))
        for c in range(8):
            kb.op("pe", lambda e, c=c: e.matmul(PS[psB][:, :n], lhsT=ones_mean[:], rhs=xsq[:, c, :n], start=(c == 0), stop=(c == 7)),
                  reads=(rt["xsq"], rC), writes=(rPS[psB],), inc=(c == 7))
        kb.op("act", lambda e: e.copy(out=msb[:, :n], in_=PS[psA][:, :n]), reads=(rPS[psA],), writes=(rt["msb"],))
        kb.op("dve", lambda e: e.tensor_tensor(out=rs[:, :n], in0=msb[:, :n], in1=msb[:, :n], op=ALU.mult), reads=(rt["msb"],), writes=(rt["rs"],))
        kb.op("dve", lambda e: e.tensor_tensor(out=rs[:, :n], in0=PS[psB][:, :n], in1=rs[:, :n], op=ALU.subtract), reads=(rPS[psB], rt["rs"]), writes=(rt["rs"],))
        kb.op("act", lambda e: e.activation(out=rs[:, :n], in_=rs[:, :n], func=AF.Sqrt, bias=epst[:, 0:1], scale=1.0), reads=(rt["rs"], rC), writes=(rt["rs"],))
        kb.op("dve", lambda e: e.reciprocal(out=rs[:, :n], in_=rs[:, :n]), reads=(rt["rs"],), writes=(rt["rs"],))
        for c in range(8):
            kb.op("dve", lambda e, c=c: e.tensor_tensor(out=tmp[:, c, :n], in0=xin[:, c, :n], in1=msb[:, :n], op=ALU.subtract), reads=(rxin, rt["msb"]), writes=(rt["tmp"],))
            kb.op("dve", lambda e, c=c: e.scalar_tensor_tensor(out=tmp[:, c, :n], in0=tmp[:, c, :n], scalar=colsb[:, l, gcol + c:gcol + c + 1], in1=rs[:, :n], op0=ALU.mult, op1=ALU.mult),
                  reads=(rt["tmp"], rt["rs"], rC), writes=(rt["tmp"],))
            if yf is not None:
                kb.op("act", lambda e, c=c: e.activation(out=yf[:, c, :n], in_=tmp[:, c, :n], func=AF.Identity, bias=colsb[:, l, bcol + c:bcol + c + 1], scale=1.0),
                      reads=(rt["tmp"], rC), writes=(ryf,))
            if yb is not None:
                kb.op("act", lambda e, c=c: e.activation(out=yb[:, c, :n], in_=tmp[:, c, :n], func=AF.Identity, bias=colsb[:, l, bcol + c:bcol + c + 1], scale=1.0),
                      reads=(rt["tmp"], rC), writes=(ryb,))

    def ln_tmps(es, n):
        d = {"xb": sb(es, "ln_xb", [128, 8, n], BF), "xsq": sb(es, "ln_xsq", [128, 8, n], BF),
             "msb": sb(es, "ln_msb", [128, n], F32), "rs": sb(es, "ln_rs", [128, n], F32)}
        d["r"] = {k: kb.res("ln_" + k) for k in ("xb", "xsq", "msb", "rs")}
        return d

    def phase_a(J, l):
        dst = J.SHA[3:3 + J.TCH] if not J.dyn else SHL
        rdst = J.rSHA if not J.dyn else rSHL
        with ExitStack() as es:
            W = sb(es, "a_win", [128, 8, ZC], BF); rW = kb.res("a_win", True)
            Wq = sb(es, "a_wuq", [128, 2, 512], BF)
            Wkv = sb(es, "a_wukv", [128, 512], BF)
            Wsw = sb(es, "a_swt", [128, 512], BF)
            CS = sb(es, "a_cs", [128, 2, 512], BF)
            lgb = sb(es, "a_lgb", [128, 2, 256], F32)
            sbb = sb(es, "a_sbb", [64, 512], F32)
            kb.dma("sp", W[:], WINb[l], rWb, rW)
            kb.dma("sp", Wq[:], WUQb[l], rWb, rW)
            kb.dma("sp", Wkv[:], WUKVb[l], rWb, rW)
            kb.dma("sp", Wsw[:], SWTb[l], rWb, rW)
            kb.dma("sp", CS[:], cst, R_in, rW)
            kb.dma("sp", lgb[:, 0, :], sgb[l, 0:1, 0:256].partition_broadcast(128), R_in, rW)
            kb.dma("sp", lgb[:, 1, :], sgb[l, 1:2, 0:256].partition_broadcast(128), R_in, rW)
            kb.dma("sp", sbb[:], sgb[l, 2:3, :].partition_broadcast(64), R_in, rW)
            lt = ln_tmps(es, 512)
            xtok = [sb(es, "a_xtok%d" % i, [128, D], F32) for i in range(2)]
            rxtok = [kb.res("a_xtok%d" % i, True) for i in range(2)]
            xraw = sb(es, "a_xraw", [128, 8, 512], F32); rxraw = kb.res("a_xraw", True)
            xf = sb(es, "a_xf", [128, 8, 512], F32); rxf = kb.res("a_xf", True)
            xb = sb(es, "a_xb", [128, 8, 512], BF); rxb = kb.res("a_xb")
            rope_t = sb(es, "a_rope", [32, 2, 512], F32); rrope = kb.res("a_rope", True)
            ev = [sb(es, "a_ev%d" % i, [128, 512], BF) for i in range(3)]
            rev = [kb.res("a_ev%d" % i) for i in range(3)]
            evf = [sb(es, "a_evf%d" % i, [128, 512], F32) for i in range(2)]
            revf = [kb.res("a_evf%d" % i) for i in range(2)]
            fbT = sb(es, "a_fbT", [128, 2, 512], BF); rfbT = kb.res("a_fbT")
            cqn = sb(es, "a_cqn", [128, 2, 512], BF); rcqn = kb.res("a_cqn")
            cqf = sb(es, "a_cqf", [128, 2, 512], F32); rcqf = kb.res("a_cqf")
            ckn = sb(es, "a_ckn", [128, 512], BF); rckn = kb.res("a_ckn")
            ckf = sb(es, "a_ckf", [128, 512], F32); rckf = kb.res("a_ckf")
            sqb = sb(es, "a_sqb", [128, 2, 512], BF); rsqb = kb.res("a_sqb")
            rstd = sb(es, "a_rstd", [128, 512], F32); rrstd = kb.res("a_rstd")
            uT = sb(es, "a_uT", [64, 4, 512], F32); ruT = kb.res("a_uT")
            krt = sb(es, "a_krt", [32, 2, 512], F32); rkrt = kb.res("a_krt")
            gv = sb(es, "a_gv", [128, 256], F32); rgv = kb.res("a_gv")
            vn = sb(es, "a_vn", [128, 256], BF); rvn = kb.res("a_vn")
            st6 = sb(es, "a_st6", [128, 4, 6], F32); rst6 = kb.res("a_st6")
            mv = sb(es, "a_mv", [128, 4, 2], F32); rmv = kb.res("a_mv")
            vat = sb(es, "a_vat", [128, 256], BF); rvat = kb.res("a_vat")
            vmt = sb(es, "a_vmt", [128, 4, 65], BF); rvmt = kb.res("a_vmt")
            ytile = sb(es, "a_ytile", [128, 512], BF); rytile = kb.res("a_ytile")
            odt = sb(es, "a_odt", [64, 4, 128], BF); rodt = kb.res("a_odt")
            odf = sb(es, "a_odf", [64, 4, 128], F32); rodf = kb.res("a_odf")
            kb.op("dve", lambda e: e.memset(vmt[:], 1.0), writes=(rvmt,))
            evi = [0]

            def nev():
                evi[0] = (evi[0] + 1) % 3
                return ev[evi[0]], rev[evi[0]]
            pb = [0]

            def nps():
                pb[0] = (pb[0] + 1) % 6
                return pb[0]

            def fm_mm(bank, c0, m, nparts=None):
                for k in range(8):
                    kb.op("pe", lambda e, k=k: e.matmul(PS[bank][:m, :], lhsT=W[:, k, c0:c0 + m], rhs=xb[:, k, :], start=(k == 0), stop=(k == 7)),
                          reads=(rW, rxb), writes=(rPS[bank],), inc=(k == 7))

            import os as _os
            ksub = int(_os.environ.get("KSUB", "99"))

            def blockbody(b):
                t0 = b * 512
                if l == 0:
                    for t in range(4):
                        xt = xtok[t % 2]; rxt = rxtok[t % 2]
                        kb.dma("sp", xt[:], J.x[t0 + t * 128:t0 + (t + 1) * 128, :], R_in, rxt)
                        for c in range(8):
                            bank = c % 6 if False else None
                        for c0 in range(0, 8, 4):
                            bk = nps()
                            for cc in range(4):
                                c = c0 + cc
                                kb.op("pe", lambda e, c=c, cc=cc, bk=bk, xt=xt: e.transpose(PS[bk][:, cc * 128:(cc + 1) * 128], xt[:, c * 128:(c + 1) * 128], identf[:]),
                                      reads=(rxt, rC), writes=(rPS[bk],), inc=(cc == 3))
                            kb.op("act", lambda e, c0=c0, bk=bk, t=t: e.copy(out=xraw[:, c0:c0 + 4, t * 128:(t + 1) * 128],
                                                                            in_=PS[bk][:, :].rearrange("p (c n) -> p c n", c=4)),
                                  reads=(rPS[bk],), writes=(rxraw,))
                    layer_norm(lt, xraw, rxraw, 512, C_EG, C_EB, 0, xf, rxf, xb, rxb, 6, 7, "emb")
                    kb.dma("pool", J.XT[:, :, t0:t0 + 512].rearrange("c p n -> p c n"), xf[:], rxf, J.rXT, waw=False)
                else:
                    kb.dma("sp", xf[:], J.XT[:, :, t0:t0 + 512].rearrange("c p n -> p c n"), J.rXT, rxf)
                    kb.op("act", lambda e: e.copy(out=xb[:], in_=xf[:]), reads=(rxf,), writes=(rxb,))
                kb.dma("sp", rope_t[:], J.rope[:, :, t0:t0 + 512].rearrange("a p n -> p a n"), R_in, rrope)

                if ksub <= 1:
                    return
                for i in range(2):
                    bk = nps(); fm_mm(bk, i * 128, 128)
                    t_, rt_ = nev()
                    kb.op("act", lambda e, bk=bk, t_=t_: e.activation(out=t_[:], in_=PS[bk][:], func=AF.Copy, scale=0.125), reads=(rPS[bk],), writes=(rt_,))
                    kb.dma("pool", J.QAT[i * 128:(i + 1) * 128, t0:t0 + 512], t_[:], rt_, J.rQAT, waw=False)
                for i in range(2):
                    bk = nps(); fm_mm(bk, 256 + i * 128, 128)
                    t_, rt_ = nev()
                    kb.op("dve", lambda e, bk=bk, t_=t_: e.tensor_copy(out=t_[:], in_=PS[bk][:]), reads=(rPS[bk],), writes=(rt_,))
                    kb.dma("pool", dst[b * 4:(b + 1) * 4, O_KAT + i * 16384:O_KAT + (i + 1) * 16384].rearrange("c (p k) -> p c k", p=128),
                           t_[:].rearrange("p (c k) -> p c k", c=4), rt_, rdst, waw=False)
                if ksub <= 2:
                    return
                for i in range(2):
                    bk = nps(); fm_mm(bk, 512 + i * 128, 128)
                    kb.op("act", lambda e, bk=bk, i=i: e.copy(out=fbT[:, i, :], in_=PS[bk][:]), reads=(rPS[bk],), writes=(rfbT,))
                for t in range(4):
                    bk = nps()
                    for i in range(2):
                        kb.op("pe", lambda e, i=i, t=t, bk=bk: e.matmul(PS[bk][:], lhsT=fbT[:, i, t * 128:(t + 1) * 128], rhs=CS[:, i, :], start=(i == 0), stop=(i == 1)),
                              reads=(rfbT, rW), writes=(rPS[bk],), inc=(i == 1))
                    kb.op("dve", lambda e, bk=bk: e.tensor_copy(out=ytile[:], in_=PS[bk][:]), reads=(rPS[bk],), writes=(rytile,))
                    kb.dma("pool", dst[b * 4 + t, O_Y:O_Y + 65536].rearrange("(g p c) -> p g c", g=4, p=128),
                           ytile[:].rearrange("p (g c) -> p g c", g=4), rytile, rdst, waw=False)
                if ksub <= 3:
                    return
                for i in range(2):
                    bk = nps(); fm_mm(bk, 768 + i * 128, 128)
                    kb.op("act", lambda e, bk=bk, i=i: e.copy(out=cqf[:, i, :], in_=PS[bk][:]), reads=(rPS[bk],), writes=(rcqf,))
                    kb.op("act", lambda e, bk=bk, i=i: e.activation(out=sqb[:, i, :], in_=PS[bk][:], func=AF.Square), reads=(rPS[bk],), writes=(rsqb,))
                bk = nps()
                for i in range(2):
                    kb.op("pe", lambda e, i=i, bk=bk: e.matmul(PS[bk][:], lhsT=onesb[:], rhs=sqb[:, i, :], start=(i == 0), stop=(i == 1)),
                          reads=(rsqb, rC), writes=(rPS[bk],), inc=(i == 1))
                kb.op("act", lambda e, bk=bk: e.activation(out=rstd[:], in_=PS[bk][:], func=AF.Sqrt, bias=epst[:, 1:2], scale=1.0 / 256), reads=(rPS[bk], rC), writes=(rrstd,))
                kb.op("dve", lambda e: e.reciprocal(out=rstd[:], in_=rstd[:]), reads=(rrstd,), writes=(rrstd,))
                for i in range(2):
                    kb.op("dve", lambda e, i=i: e.scalar_tensor_tensor(out=cqn[:, i, :], in0=cqf[:, i, :], scalar=colsb[:, l, C_QG + i:C_QG + i + 1], in1=rstd[:], op0=ALU.mult, op1=ALU.mult),
                          reads=(rcqf, rrstd, rC), writes=(rcqn,))
                for h in range(4):
                    bk = nps()
                    for i in range(2):
                        kb.op("pe", lambda e, i=i, bk=bk, h=h: e.matmul(PS[bk][:64, :], lhsT=Wq[:, i, h * 128:h * 128 + 64], rhs=cqn[:, i, :], start=(i == 0), stop=(i == 1)),
                              reads=(rW, rcqn), writes=(rPS[bk],), inc=(i == 1))
                    t_, rt_ = nev()
                    kb.op("act", lambda e, bk=bk, t_=t_: e.copy(out=t_[:64, :], in_=PS[bk][:64, :]), reads=(rPS[bk],), writes=(rt_,))
                    kb.dma("pool", J.QMT[h, 0:64, t0:t0 + 512], t_[:64, :], rt_, J.rQMT, waw=False)
                    bk1 = nps()
                    for i in range(2):
                        kb.op("pe", lambda e, i=i, bk1=bk1, h=h: e.matmul(PS[bk1][:32, :], lhsT=Wq[:, i, h * 128 + 64:h * 128 + 96], rhs=cqn[:, i, :], start=(i == 0), stop=(i == 1)),
                              reads=(rW, rcqn), writes=(rPS[bk1],), inc=(i == 1))
                    bk2 = nps()
                    for i in range(2):
                        kb.op("pe", lambda e, i=i, bk2=bk2, h=h: e.matmul(PS[bk2][:32, :], lhsT=Wq[:, i, h * 128 + 96:h * 128 + 128], rhs=cqn[:, i, :], start=(i == 0), stop=(i == 1)),
                              reads=(rW, rcqn), writes=(rPS[bk2],), inc=(i == 1))
                    kb.op("dve", lambda e, bk1=bk1: e.tensor_tensor(out=krt[:, 0, :], in0=PS[bk1][:32, :], in1=rope_t[:, 0, :], op=ALU.mult), reads=(rPS[bk1], rrope), writes=(rkrt,))
                    kb.op("dve", lambda e, bk2=bk2: e.tensor_tensor(out=krt[:, 1, :], in0=PS[bk2][:32, :], in1=rope_t[:, 1, :], op=ALU.mult), reads=(rPS[bk2], rrope), writes=(rkrt,))
                    t_, rt_ = nev()
                    kb.op("dve", lambda e, t_=t_: e.tensor_tensor(out=t_[:32, :], in0=krt[:, 0, :], in1=krt[:, 1, :], op=ALU.add), reads=(rkrt,), writes=(rt_,))
                    kb.dma("pool", J.QMT[h, 64:96, t0:t0 + 512], t_[:32, :], rt_, J.rQMT, waw=False)
                if ksub <= 4:
                    return
                bk = nps(); fm_mm(bk, 1024, 128)
                kb.op("act", lambda e, bk=bk: e.copy(out=ckf[:], in_=PS[bk][:]), reads=(rPS[bk],), writes=(rckf,))
                kb.op("act", lambda e, bk=bk: e.activation(out=sqb[:, 0, :], in_=PS[bk][:], func=AF.Square), reads=(rPS[bk],), writes=(rsqb,))
                bk = nps()
                kb.op("pe", lambda e, bk=bk: e.matmul(PS[bk][:], lhsT=onesb[:], rhs=sqb[:, 0, :], start=True, stop=True), reads=(rsqb, rC), writes=(rPS[bk],))
                kb.op("act", lambda e, bk=bk: e.activation(out=rstd[:], in_=PS[bk][:], func=AF.Sqrt, bias=epst[:, 1:2], scale=1.0 / 128), reads=(rPS[bk], rC), writes=(rrstd,))
                kb.op("dve", lambda e: e.reciprocal(out=rstd[:], in_=rstd[:]), reads=(rrstd,), writes=(rrstd,))
                kb.op("dve", lambda e: e.scalar_tensor_tensor(out=ckn[:], in0=ckf[:], scalar=colsb[:, l, C_KVG:C_KVG + 1], in1=rstd[:], op0=ALU.mult, op1=ALU.mult),
                      reads=(rckf, rrstd, rC), writes=(rckn,))
                for h in range(4):
                    bk = nps()
                    kb.op("pe", lambda e, bk=bk, h=h: e.matmul(PS[bk][:64, :], lhsT=Wkv[:, h * 64:(h + 1) * 64], rhs=ckn[:], start=True, stop=True), reads=(rW, rckn), writes=(rPS[bk],))
                    t_, rt_ = nev()
                    kb.op("act", lambda e, bk=bk, t_=t_: e.copy(out=t_[:64, :], in_=PS[bk][:64, :]), reads=(rPS[bk],), writes=(rt_,))
                    kb.dma("pool", dst[b * 4:(b + 1) * 4, O_KMT:O_KMT + 64 * 512].rearrange("c (p hk) -> p c hk", p=64)[:, :, h * 128:(h + 1) * 128],
                           t_[:64, :].rearrange("p (c k) -> p c k", c=4), rt_, rdst, waw=False)
                for t in range(4):
                    bk = nps()
                    kb.op("pe", lambda e, bk=bk, t=t: e.matmul(PS[bk][:, :256], lhsT=ckn[:, t * 128:(t + 1) * 128], rhs=Wkv[:, 256:512], start=True, stop=True), reads=(rW, rckn), writes=(rPS[bk],))
                    kb.op("dve", lambda e, bk=bk: e.tensor_copy(out=vmt[:, :, 0:64], in_=PS[bk][:, :256].rearrange("p (h d) -> p h d", h=4)), reads=(rPS[bk],), writes=(rvmt,))
                    kb.dma("pool", dst[b * 4 + t, O_VM:O_VM + 128 * 260].rearrange("(p f) -> p f", p=128), vmt[:].rearrange("p h d -> p (h d)"), rvmt, rdst, waw=False)
                if ksub <= 5:
                    return
                bk1 = nps(); fm_mm(bk1, 1152, 32)
                bk2 = nps(); fm_mm(bk2, 1184, 32)
                kb.op("dve", lambda e, bk1=bk1: e.tensor_tensor(out=krt[:, 0, :], in0=PS[bk1][:32, :], in1=rope_t[:, 0, :], op=ALU.mult), reads=(rPS[bk1], rrope), writes=(rkrt,))
                kb.op("dve", lambda e, bk2=bk2: e.tensor_tensor(out=krt[:, 1, :], in0=PS[bk2][:32, :], in1=rope_t[:, 1, :], op=ALU.mult), reads=(rPS[bk2], rrope), writes=(rkrt,))
                t_, rt_ = nev()
                kb.op("dve", lambda e, t_=t_: e.tensor_tensor(out=t_[:32, :], in0=krt[:, 0, :], in1=krt[:, 1, :], op=ALU.add), reads=(rkrt,), writes=(rt_,))
                for h in range(4):
                    kb.dma("pool", dst[b * 4:(b + 1) * 4, O_KMT + 64 * 512:O_KMT + 96 * 512].rearrange("c (p hk) -> p c hk", p=32)[:, :, h * 128:(h + 1) * 128],
                           t_[:32, :].rearrange("p (c k) -> p c k", c=4), rt_, rdst, waw=False)
                if ksub <= 6:
                    return
                for h in range(4):
                    bk = nps(); fm_mm(bk, 1216 + h * 64, 64)
                    kb.op("act", lambda e, bk=bk, h=h: e.activation(out=uT[:, h, :], in_=PS[bk][:64, :], func=AF.Gelu_apprx_tanh), reads=(rPS[bk],), writes=(ruT,))
                if ksub <= 7:
                    return
                for t in range(4):
                    bk = nps()
                    for k in range(8):
                        kb.op("pe", lambda e, k=k, bk=bk, t=t: e.matmul(PS[bk][:], lhsT=xb[:, k, t * 128:(t + 1) * 128], rhs=W[:, k, 1472:1984], start=(k == 0), stop=(k == 7)),
                              reads=(rW, rxb), writes=(rPS[bk],), inc=(k == 7))
                    ks2 = int(_os.environ.get("KSUB2", "9"))
                    if ks2 <= 1:
                        continue
                    kb.op("dve", lambda e, bk=bk: e.tensor_copy(out=vat[:], in_=PS[bk][:, 0:256]), reads=(rPS[bk],), writes=(rvat,))
                    if ks2 <= 2:
                        continue
                    kb.dma("pool", dst[b * 4 + t, O_VA:O_VA + 32768].rearrange("(p f) -> p f", p=128), vat[:], rvat, rdst, waw=False)
                    if ks2 <= 3:
                        continue
                    kb.op("dve", lambda e, bk=bk: e.tensor_copy(out=gv[:], in_=PS[bk][:, 256:512]), reads=(rPS[bk],), writes=(rgv,))
                    kb.op("act", lambda e, bk=bk: e.activation(out=gv[:], in_=gv[:], func=AF.Gelu_apprx_tanh), reads=(rgv,), writes=(rgv,))
                    if ksub <= 8:
                        continue
                    for h in range(4):
                        kb.op("dve", lambda e, h=h: e.bn_stats(out=st6[:, h, :], in_=gv[:, h * 64:(h + 1) * 64]), reads=(rgv,), writes=(rst6,))
                        kb.op("dve", lambda e, h=h: e.bn_aggr(out=mv[:, h, :], in_=st6[:, h, :]), reads=(rst6,), writes=(rmv,))
                    kb.op("act", lambda e: e.activation(out=mv[:, :, 1], in_=mv[:, :, 1], func=AF.Sqrt, bias=epst[:, 0:1], scale=1.0), reads=(rmv, rC), writes=(rmv,))
                    kb.op("dve", lambda e: e.reciprocal(out=mv[:, :, 1], in_=mv[:, :, 1]), reads=(rmv,), writes=(rmv,))
                    for h in range(4):
                        kb.op("dve", lambda e, h=h: e.tensor_scalar(out=gv[:, h * 64:(h + 1) * 64], in0=gv[:, h * 64:(h + 1) * 64], scalar1=mv[:, h, 0:1], scalar2=mv[:, h, 1:2], op0=ALU.subtract, op1=ALU.mult),
                              reads=(rgv, rmv), writes=(rgv,))
                    if ksub <= 9:
                        continue
                    kb.op("dve", lambda e: e.tensor_tensor(out=gv[:], in0=gv[:], in1=lgb[:, 0, :], op=ALU.mult), reads=(rgv, rW), writes=(rgv,))
                    kb.op("dve", lambda e: e.tensor_tensor(out=vn[:], in0=gv[:], in1=lgb[:, 1, :], op=ALU.add), reads=(rgv, rW), writes=(rvn,))
                    if ksub <= 10:
                        continue
                    bk = nps()
                    for h in range(4):
                        kb.op("pe", lambda e, h=h, bk=bk: e.matmul(PS[bk][:64, h * 128:(h + 1) * 128], lhsT=vn[:, h * 64:(h + 1) * 64], rhs=Wsw[:, h * 128:(h + 1) * 128], start=True, stop=True),
                              reads=(rW, rvn), writes=(rPS[bk],), inc=(h == 3))
                    kb.op("dve", lambda e, bk=bk: e.tensor_tensor(out=odf[:].rearrange("p h n -> p (h n)"), in0=PS[bk][:64, :], in1=sbb[:], op=ALU.add), reads=(rPS[bk], rW), writes=(rodf,))
                    kb.op("dve", lambda e, t=t: e.tensor_tensor(out=odt[:], in0=odf[:], in1=uT[:, :, t * 128:(t + 1) * 128], op=ALU.mult), reads=(rodf, ruT), writes=(rodt,))
                    kb.dma("pool", J.MIXT[768:1024, t0 + t * 128:t0 + (t + 1) * 128].rearrange("(h p) n -> p h n", h=4), odt[:], rodt, J.rMIXT, waw=False)
            for b in range(J.NB):
                blockbody(b)
            kb.barrier()

    ccsem = nc.alloc_semaphore("cc")
    ccn = [0]

    def all_gather(src_ap, dst_ap, rsrc, rdst):
        kb.barrier()
        ccn[0] += 1
        nc.gpsimd.collective_compute("AllGather", ALU.bypass, replica_groups=[list(range(NCORE))],
                                     ins=[src_ap], outs=[dst_ap]).then_inc(ccsem, 1)
        for e in kb.eng:
            kb.eng[e].wait_ge(ccsem, ccn[0])

    def phase_natten(J, l):
        with ExitStack() as es:
            BT = sb(es, "n_bt", [128, 35, 512], BF); rBT = kb.res("n_bt", True)
            kb.dma("sp", BT[:], J.BTb[l], rWb, rBT)
            KT = [sb(es, "n_kt%d" % i, [64, 4, 10, 128], BF) for i in range(2)]
            rKT = [kb.res("n_kt%d" % i, True) for i in range(2)]
            VA = [sb(es, "n_va%d" % i, [128, 10, 256], BF) for i in range(2)]
            rVA = [kb.res("n_va%d" % i, True) for i in range(2)]
            QT = [sb(es, "n_qt%d" % i, [64, 4, 512], BF) for i in range(2)]
            rQT = [kb.res("n_qt%d" % i, True) for i in range(2)]
            PT = [sb(es, "n_pt%d" % i, [128, 512], BF) for i in range(3)]
            rPT = [kb.res("n_pt%d" % i) for i in range(3)]
            OA = [sb(es, "n_oa%d" % i, [64, 4, 512], BF) for i in range(2)]
            rOA = [kb.res("n_oa%d" % i) for i in range(2)]
            rr = sb(es, "n_rr", [64, 512], F32); rrr = kb.res("n_rr")
            pti = [0]
            if J.dyn:
                kb.dma("sp", LWIN[:, :], J.SHA[bass.ds(getpid("sp") * J.NCH, J.NCH + 6), 0:65536], J.rSHA, rLWIN)
                SRC, rSRC = LWIN, rLWIN
            else:
                SRC, rSRC = J.SHA, J.rSHA
            cbase = 0

            def load(b):
                i = b % 2
                g0 = cbase + b * 4
                sl = slice(g0, g0 + 10)
                for h_ in range(4):
                    kb.dma("sp", KT[i][:, h_, :, :],
                           SRC[sl, O_KAT + h_ * 8192:O_KAT + (h_ + 1) * 8192].rearrange("c (p k) -> p c k", p=64),
                           rSRC, rKT[i])
                kb.dma("sp", VA[i][:], SRC[sl, O_VA:O_VA + 32768].rearrange("c (p f) -> p c f", p=128), rSRC, rVA[i])
                kb.dma("sp", QT[i][:], J.QAT[:, b * 512:(b + 1) * 512].rearrange("(h p) n -> p h n", h=4), J.rQAT, rQT[i])
            units = []
            for b in range(J.NB):
                for t in range(4):
                    il = b * 4 + t
                    slot = 0 if il == 0 else 1 if il == 1 else 3 if il == J.NCH - 2 else 4 if il == J.NCH - 1 else 2
                    js = list(range(-2, 3)) if slot == 2 else list(range(-3, 4))
                    for j in js:
                        units.append((b, t, slot, j, j == js[0], j == js[-1]))

            def emit_s(ui):
                b, t, slot, j, jf, jl = units[ui]
                bi = b % 2
                cj = t + j + 3
                sbk = ui % 3
                pt = PT[ui % 3]; rpt = rPT[ui % 3]
                kb.op("pe", lambda e: e.matmul(PS[sbk][:], lhsT=identb[:], rhs=BT[:, slot * 7 + j + 3, :], start=True, stop=False),
                      reads=(rBT, rC), writes=(rPS[sbk],), inc=False)
                for h in range(4):
                    kb.op("pe", lambda e, h=h: e.matmul(PS[sbk][:, h * 128:(h + 1) * 128], lhsT=KT[bi][:, h, cj, :],
                                                        rhs=QT[bi][:, h, t * 128:(t + 1) * 128], start=False, stop=True),
                          reads=(rKT[bi], rQT[bi]), writes=(rPS[sbk],), inc=(h == 3))
                kb.op("act", lambda e: e.activation(out=pt[:], in_=PS[sbk][:], func=AF.Exp), reads=(rPS[sbk],), writes=(rpt,))

            def emit_pv(ui):
                b, t, slot, j, jf, jl = units[ui]
                bi = b % 2
                cj = t + j + 3
                til = b * 4 + t
                OB = 3 + (til % 2); RB = 5 + (til % 2)
                pt = PT[ui % 3]; rpt = rPT[ui % 3]
                if jf:
                    kb.op("pe", lambda e: e.matmul(PS[OB][:64, :], lhsT=zerob[:, 0:64], rhs=BT[:, 0, :], start=True, stop=False),
                          reads=(rBT, rC), writes=(rPS[OB],), inc=False)
                for h in range(4):
                    kb.op("pe", lambda e, h=h: e.matmul(PS[OB][:64, h * 128:(h + 1) * 128], lhsT=VA[bi][:, cj, h * 64:(h + 1) * 64],
                                                        rhs=pt[:, h * 128:(h + 1) * 128], start=False, stop=jl),
                          reads=(rVA[bi], rpt), writes=(rPS[OB],), inc=False)
                kb.op("pe", lambda e: e.matmul(PS[RB][:64, :], lhsT=onesb[:, 0:64], rhs=pt[:], start=jf, stop=jl),
                      reads=(rpt, rC), writes=(rPS[RB],), inc=True)
                if jl:
                    kb.op("dve", lambda e: e.reciprocal(out=rr[:], in_=PS[RB][:64, :]), reads=(rPS[RB],), writes=(rrr,))
                    kb.op("dve", lambda e: e.tensor_tensor(out=OA[bi][:, :, t * 128:(t + 1) * 128], in0=PS[OB][:64, :].rearrange("p (h n) -> p h n", h=4),
                                                            in1=rr[:].rearrange("p (h n) -> p h n", h=4), op=ALU.mult),
                          reads=(rPS[OB], rrr), writes=(rOA[bi],))
                    if t == 3:
                        kb.dma("pool", J.MIXT[0:256, b * 512:(b + 1) * 512].rearrange("(h p) n -> p h n", h=4), OA[bi][:], rOA[bi], J.rMIXT, waw=False)
                        if b + 2 < J.NB:
                            load(b + 2)

            load(0)
            if J.NB > 1:
                load(1)
            emit_s(0)
            for ui in range(len(units)):
                if ui + 1 < len(units):
                    emit_s(ui + 1)
                emit_pv(ui)
            kb.barrier()

    def phase_fourier(J, l):
        N1, NK2 = J.N1, J.NK2
        nch1 = 512 // (2 * N1)
        nk1 = 512 // NK2
        scale = 1.0 / math.sqrt(64.0 * J.T)
        with ExitStack() as es:
            F1 = sb(es, "f_f1", [N1, 2, 2 * N1], BF); rF = kb.res("f_tab", True)
            G3 = sb(es, "f_g3", [128, 2, N1 * NK2], BF)
            kb.dma("sp", F1[:], J.f1.rearrange("a p n -> p a n"), R_in, rF)
            kb.dma("sp", G3[:], J.g3.rearrange("a p n -> p a n"), R_in, rF)
            XS = [sb(es, "f_xs%d" % i, [N1, 128, 128], BF) for i in range(2)]
            rXS = [kb.res("f_xs%d" % i, True) for i in range(2)]
            A = sb(es, "f_a", [128, 2, N1, 64], BF); rA = kb.res("f_a")
            XS2 = sb(es, "f_xs2", [N1, 128, 128], BF); rXS2 = kb.res("f_xs2")
            OBT = [sb(es, "f_obt%d" % i, [64, J.L], BF) for i in range(2)]
            rOBT = [kb.res("f_obt%d" % i) for i in range(2)]
            pb = [0]

            def nps():
                pb[0] = (pb[0] + 1) % 8
                return pb[0]

            def load(g):
                kb.dma("sp", XS[g % 2][:], J.SHA[3:3 + J.TCH, O_Y + g * 16384:O_Y + (g + 1) * 16384].rearrange("c (t f) -> c t f", t=128), J.rSHA, rXS[g % 2])
            load(0)
            for g in range(4):
                if g < 3:
                    load(g + 1)
                xs0 = XS[g % 2]; rxs0 = rXS[g % 2]
                kb.op("pool", lambda e, xs0=xs0: e.tensor_copy(out=XS2[:], in_=xs0[:].rearrange("p t f -> p f t")), reads=(rxs0,), writes=(rXS2,))
                xs = XS2; rxs = rXS2
                for c0 in range(0, 64, nch1):
                    bk = nps()
                    for cc in range(nch1):
                        c = c0 + cc
                        kb.op("pe", lambda e, c=c, cc=cc, bk=bk: e.matmul(PS[bk][:, cc * 2 * N1:(cc + 1) * 2 * N1], lhsT=xs[:, c, :], rhs=F1[:, 0, :], start=True, stop=False),
                              reads=(rxs, rF), writes=(rPS[bk],), inc=False)
                        kb.op("pe", lambda e, c=c, cc=cc, bk=bk: e.matmul(PS[bk][:, cc * 2 * N1:(cc + 1) * 2 * N1], lhsT=xs[:, 64 + c, :], rhs=F1[:, 1, :], start=False, stop=True),
                              reads=(rxs, rF), writes=(rPS[bk],), inc=(cc == nch1 - 1))
                    eng = "dve"
                    for ri in range(2):
                        src = PS[bk][:, :nch1 * 2 * N1].rearrange("p (c r k) -> p c r k", c=nch1, r=2)[:, :, ri, :]
                        dstv = A[:, ri, :, c0:c0 + nch1].rearrange("p k c -> p c k")
                        if eng == "act":
                            kb.op("act", lambda e, src=src, dstv=dstv: e.copy(out=dstv, in_=src), reads=(rPS[bk],), writes=(rA,))
                        else:
                            kb.op("dve", lambda e, src=src, dstv=dstv: e.tensor_copy(out=dstv, in_=src), reads=(rPS[bk],), writes=(rA,))
                obt = OBT[g % 2]; robt = rOBT[g % 2]
                for k0 in range(0, N1, nk1):
                    bk = nps()
                    nk = min(nk1, N1 - k0)
                    for kk in range(nk):
                        k1 = k0 + kk
                        kb.op("pe", lambda e, k1=k1, kk=kk, bk=bk: e.matmul(PS[bk][:64, kk * NK2:(kk + 1) * NK2], lhsT=A[:, 0, k1, :], rhs=G3[:, 0, k1 * NK2:(k1 + 1) * NK2], start=True, stop=False),
                              reads=(rA, rF), writes=(rPS[bk],), inc=False)
                        kb.op("pe", lambda e, k1=k1, kk=kk, bk=bk: e.matmul(PS[bk][:64, kk * NK2:(kk + 1) * NK2], lhsT=A[:, 1, k1, :], rhs=G3[:, 1, k1 * NK2:(k1 + 1) * NK2], start=False, stop=True),
                              reads=(rA, rF), writes=(rPS[bk],), inc=(kk == nk - 1))
                    src = PS[bk][:64, :nk * NK2].rearrange("p (k n) -> p k n", k=nk)
                    dstv = obt[:, :].rearrange("p (n k) -> p k n", k=N1)[:, k0:k0 + nk, :]
                    kb.op("dve", lambda e, src=src, dstv=dstv: e.tensor_scalar(out=dstv, in0=src, scalar1=scale, scalar2=None, op0=ALU.mult), reads=(rPS[bk],), writes=(robt,))
                kb.dma("pool", J.MIXT[256 + g * 64:256 + (g + 1) * 64, :], obt[:], robt, J.rMIXT, waw=False)
            kb.barrier()

    def phase_mla(J, l):
        sc = 96.0 ** -0.5
        NG = J.TCH // 4
        with ExitStack() as es:
            KM = [sb(es, "m_km%d" % i, [96, 4, 4, 128], BF) for i in range(3)]
            rKM = [kb.res("m_km%d" % i, True) for i in range(3)]
            VM = [sb(es, "m_vm%d" % i, [128, 4, 260], BF) for i in range(3)]
            rVM = [kb.res("m_vm%d" % i, True) for i in range(3)]
            QM = [sb(es, "m_qm%d" % i, [96, 4, 512], BF) for i in range(2)]
            rQM = [kb.res("m_qm%d" % i, True) for i in range(2)]
            PT = [sb(es, "m_pt%d" % i, [128, 512], BF) for i in range(3)]
            rPT = [kb.res("m_pt%d" % i) for i in range(3)]
            rr = [sb(es, "m_rr%d" % i, [64, 512], F32) for i in range(2)]; rrr = [kb.res("m_rr%d" % i) for i in range(2)]
            OC = [sb(es, "m_oc%d" % i, [64, 512], BF) for i in range(2)]
            rOC = [kb.res("m_oc%d" % i) for i in range(2)]
            steps = [(b, hp, g) for b in range(J.NB) for hp in range(2) for g in range(NG)]

            def load(si):
                b, hp, g = steps[si]
                i = si % 3
                kb.dma("sp", KM[i][:].rearrange("e c h k -> e c (h k)"), J.SHA[3 + g * 4:3 + (g + 1) * 4, O_KMT:O_KMT + 49152].rearrange("c (e hk) -> e c hk", e=96), J.rSHA, rKM[i])
                kb.dma("sp", VM[i][:], J.SHA[3 + g * 4:3 + (g + 1) * 4, O_VM:O_VM + 128 * 260].rearrange("c (p f) -> p c f", p=128), J.rSHA, rVM[i])
                if g == 0 and hp == 0:
                    kb.dma("sp", QM[b % 2][:], J.QMT[:, :, b * 512:(b + 1) * 512].rearrange("h e n -> e h n"), J.rQMT, rQM[b % 2])
            load(0)
            if len(steps) > 1:
                load(1)
            oci = [0]
            units = [(si, cc, hh) for si in range(len(steps)) for cc in range(4) for hh in range(2)]

            def emit_s(ui):
                si, cc, hh = units[ui]
                b, hp, g = steps[si]
                i = si % 3
                h = 2 * hp + hh
                qm = QM[b % 2]; rqm = rQM[b % 2]
                sbk = ui % 3
                pt = PT[ui % 3]; rpt = rPT[ui % 3]
                kb.op("pe", lambda e: e.matmul(PS[sbk][:], lhsT=KM[i][:, cc, h, :], rhs=qm[:, h, :], start=True, stop=True),
                      reads=(rKM[i], rqm), writes=(rPS[sbk],))
                kb.op("act", lambda e: e.activation(out=pt[:], in_=PS[sbk][:], func=AF.Exp, scale=sc), reads=(rPS[sbk],), writes=(rpt,))

            def emit_pv(ui):
                si, cc, hh = units[ui]
                b, hp, g = steps[si]
                i = si % 3
                h = 2 * hp + hh
                pt = PT[ui % 3]; rpt = rPT[ui % 3]
                first = (g == 0 and cc == 0); last = (g == NG - 1 and cc == 3)
                kb.op("pe", lambda e: e.matmul(PS[3 + hh][:64, :], lhsT=VM[i][:, cc, h * 65:h * 65 + 64], rhs=pt[:], start=first, stop=last),
                      reads=(rVM[i], rpt), writes=(rPS[3 + hh],), inc=False)
                kb.op("pe", lambda e: e.matmul(PS[5 + hh][:64, :], lhsT=onesb[:, 0:64], rhs=pt[:], start=first, stop=last),
                      reads=(rpt, rC), writes=(rPS[5 + hh],))
                if g == NG - 1 and cc == 3:
                    kb.op("dve", lambda e: e.reciprocal(out=rr[hh][:], in_=PS[5 + hh][:64, :]), reads=(rPS[5 + hh],), writes=(rrr[hh],))
                    oc = OC[oci[0] % 2]; roc = rOC[oci[0] % 2]; oci[0] += 1
                    kb.op("dve", lambda e: e.tensor_tensor(out=oc[:], in0=PS[3 + hh][:64, :], in1=rr[hh][:], op=ALU.mult), reads=(rPS[3 + hh], rrr[hh]), writes=(roc,))
                    kb.dma("pool", J.MIXT[512 + h * 64:512 + (h + 1) * 64, b * 512:(b + 1) * 512], oc[:], roc, J.rMIXT, waw=False)
                if cc == 3 and hh == 1 and si + 3 < len(steps):
                    load(si + 3)

            if len(steps) > 2:
                load(2)
            emit_s(0)
            for ui in range(len(units)):
                if ui + 1 < len(units):
                    emit_s(ui + 1)
                emit_pv(ui)
            kb.barrier()

    def phase_wout(J, l):
        with ExitStack() as es:
            W = sb(es, "o_w", [128, 8, D], BF); rW = kb.res("o_w", True)
            kb.dma("sp", W[:], WOUTb[l], rWb, rW)
            lt = ln_tmps(es, 512)
            MX = [sb(es, "o_mx%d" % i, [128, 8, 512], BF) for i in range(2)]
            rMX = [kb.res("o_mx%d" % i, True) for i in range(2)]
            XF = [sb(es, "o_xf%d" % i, [128, 8, 512], F32) for i in range(2)]
            rXF = [kb.res("o_xf%d" % i, True) for i in range(2)]
            R = sb(es, "o_r", [128, 8, 512], F32); rR = kb.res("o_r")
            Y = sb(es, "o_y", [128, 8, 512], F32); rY = kb.res("o_y")
            hb = sb(es, "o_hb", [128, 8, 2], BF); rhb = kb.res("o_hb")

            def load(b):
                i = b % 2
                kb.dma("sp", MX[i][:], J.MIXT[:, b * 512:(b + 1) * 512].rearrange("(c p) n -> p c n", p=128), J.rMIXT, rMX[i])
                kb.dma("sp", XF[i][:], J.XT[:, :, b * 512:(b + 1) * 512].rearrange("c p n -> p c n"), J.rXT, rXF[i])
            load(0)
            for b in range(J.NB):
                if b + 1 < J.NB:
                    load(b + 1)
                i = b % 2
                for m in range(8):
                    bk = m % 6
                    for k in range(8):
                        kb.op("pe", lambda e, k=k, m=m, bk=bk: e.matmul(PS[bk][:], lhsT=W[:, k, m * 128:(m + 1) * 128], rhs=MX[i][:, k, :], start=(k == 0), stop=(k == 7)),
                              reads=(rW, rMX[i]), writes=(rPS[bk],), inc=(k == 7))
                    kb.op("dve", lambda e, m=m, bk=bk: e.scalar_tensor_tensor(out=R[:, m, :], in0=XF[i][:, m, :], scalar=ALPHA, in1=PS[bk][:], op0=ALU.mult, op1=ALU.add),
                          reads=(rXF[i], rPS[bk]), writes=(rR,))
                layer_norm(lt, R, rR, 512, C_LN1G, C_LN1B, l, Y, rY, None, None, 6, 7, "ln1")
                kb.dma("pool", J.X1T[:, :, b * 512:(b + 1) * 512].rearrange("c p n -> p c n"), Y[:], rY, J.rX1T, waw=False)
                if J.dyn and (b == 0 or b == J.NB - 1):
                    if b == 0:
                        kb.op("dve", lambda e: e.tensor_copy(out=hb[:, :, 0:1], in_=Y[:, :, 0:1]), reads=(rY,), writes=(rhb,))
                    if b == J.NB - 1:
                        kb.op("dve", lambda e: e.tensor_copy(out=hb[:, :, 1:2], in_=Y[:, :, 511:512]), reads=(rY,), writes=(rhb,))
                        for r_ in range(2):
                            kb.dma("pool", HL[r_].rearrange("(c p) -> p c", p=128), hb[:, :, r_], rhb, rHL, slow=True)
            kb.barrier()

    def phase_ffn(J, l, final):
        blocks = []
        t = 0
        while t < J.L:
            n = min(510, J.L - t)
            blocks.append((t, n))
            t += n
        with ExitStack() as es:
            WD = sb(es, "c_wd", [128, 22, D], BF); rWD = kb.res("c_wd", True)
            kb.dma("sp", WD[:], WDNb[l], rWb, rWD)
            WU = [sb(es, "c_wu%d" % i, [128, 8, 2, 128], BF) for i in range(3)]
            rWU = [kb.res("c_wu%d" % i, True) for i in range(3)]
            lt = ln_tmps(es, 512)
            X1 = [sb(es, "c_x1%d" % i, [128, 8, 512], F32) for i in range(2)]
            rX1 = [kb.res("c_x1%d" % i, True) for i in range(2)]
            XB = sb(es, "c_xb", [128, 8, 512], BF); rXB = kb.res("c_xb", True)
            GT = sb(es, "c_gt", [128, 22, 512], BF); rGT = kb.res("c_gt")
            cg = [sb(es, "c_cg%d" % i, [128, 512], F32) for i in range(2)]
            rcg = [kb.res("c_cg%d" % i) for i in range(2)]
            cv = [sb(es, "c_cv%d" % i, [128, 512], F32) for i in range(2)]
            rcv = [kb.res("c_cv%d" % i) for i in range(2)]
            R = sb(es, "c_r", [128, 8, 512], F32); rR = kb.res("c_r")
            Yf = sb(es, "c_y", [128, 8, 512], F32); rY = kb.res("c_y")
            OT = [sb(es, "c_ot%d" % i, [128, D], F32) for i in range(2)]
            rOT = [kb.res("c_ot%d" % i) for i in range(2)]
            hx = sb(es, "c_hx", [128, 8, 2], BF); rhx = kb.res("c_hx", True)
            pc = [sb(es, "c_pc%d" % i, [128, 8, 1], BF) for i in range(2)]
            rpc = [kb.res("c_pc%d" % i) for i in range(2)]
            if J.dyn:
                p = getpid("sp")
                kb.dma("sp", hx[:, :, 0], HGP[bass.ds(2 * p, 1), :].rearrange("r (c p) -> p (c r)", p=128), rHGP, rhx, slow=True)
                kb.dma("sp", hx[:, :, 1], HGP[bass.ds(2 * p + 3, 1), :].rearrange("r (c p) -> p (c r)", p=128), rHGP, rhx, slow=True)
            else:
                kb.op("dve", lambda e: e.memset(hx[:], 0.0), writes=(rhx,))
            wsteps = [(bi, f) for bi in range(len(blocks)) for f in range(22)]

            def loadw(si):
                bi, f = wsteps[si]
                i = si % 3
                kb.dma("sp", WU[i][:].rearrange("p k h j -> p k (h j)"), WUPb[l, :, :, f * 256:(f + 1) * 256], rWb, rWU[i])

            def loadx(bi):
                t0, n = blocks[bi]
                i = bi % 2
                kb.dma("sp", X1[i][:, :, :n], J.X1T[:, :, t0:t0 + n].rearrange("c p n -> p c n"), J.rX1T, rX1[i])
            loadx(0)
            loadw(0); loadw(1)
            oti = [0]
            for bi, (t0, n) in enumerate(blocks):
                if bi + 1 < len(blocks):
                    loadx(bi + 1)
                i = bi % 2
                kb.op("act", lambda e, n=n, i=i: e.copy(out=XB[:, :, 1:n + 1], in_=X1[i][:, :, :n]), reads=(rX1[i],), writes=(rXB,))
                if bi == 0:
                    kb.op("dve", lambda e: e.tensor_copy(out=XB[:, :, 0:1], in_=hx[:, :, 0:1]), reads=(rhx,), writes=(rXB,))
                else:
                    kb.op("dve", lambda e, i=i: e.tensor_copy(out=XB[:, :, 0:1], in_=pc[1 - i][:]), reads=(rpc[1 - i],), writes=(rXB,))
                kb.op("dve", lambda e, i=i, n=n: e.tensor_copy(out=pc[i][:], in_=X1[i][:, :, n - 1:n]), reads=(rX1[i],), writes=(rpc[i],))
                if bi == len(blocks) - 1:
                    kb.op("dve", lambda e, n=n: e.tensor_copy(out=XB[:, :, n + 1:n + 2], in_=hx[:, :, 1:2]), reads=(rhx,), writes=(rXB,))
                else:
                    kb.op("dve", lambda e, n=n, i=i: e.tensor_copy(out=XB[:, :, n + 1:n + 2], in_=X1[1 - i][:, :, 0:1]), reads=(rX1[1 - i],), writes=(rXB,))
                m = n + 2
                for f in range(22):
                    si = bi * 22 + f
                    if si + 2 < len(wsteps):
                        loadw(si + 2)
                    wi = si % 3
                    info = []
                    for half in range(2):
                        bk = (2 * f + half) % 4
                        for k in range(8):
                            kb.op("pe", lambda e, k=k, bk=bk, half=half, wi=wi, m=m: e.matmul(PS[bk][:, :m], lhsT=WU[wi][:, k, half, :], rhs=XB[:, k, :m], start=(k == 0), stop=(k == 7)),
                                  reads=(rWU[wi], rXB), writes=(rPS[bk],), inc=(k == 7))
                        fc = f + 22 * half
                        tt = (cg if half == 0 else cv)[f % 2]; rtt = (rcg if half == 0 else rcv)[f % 2]
                        info.append((bk, fc, tt, rtt))
                    for (bk, fc, tt, rtt) in info:
                        kb.op("dve", lambda e, bk=bk, fc=fc, tt=tt, n=n: e.tensor_scalar(out=tt[:, :n], in0=PS[bk][:, 1:n + 1], scalar1=colsb[:, l, C_CW1 + fc:C_CW1 + fc + 1],
                                                                                      scalar2=colsb[:, l, C_CB + fc:C_CB + fc + 1], op0=ALU.mult, op1=ALU.add),
                              reads=(rPS[bk], rC), writes=(rtt,))
                    for (bk, fc, tt, rtt) in info:
                        kb.op("dve", lambda e, bk=bk, fc=fc, tt=tt, n=n: e.scalar_tensor_tensor(out=tt[:, :n], in0=PS[bk][:, 0:n], scalar=colsb[:, l, C_CW0 + fc:C_CW0 + fc + 1], in1=tt[:, :n], op0=ALU.mult, op1=ALU.add),
                              reads=(rPS[bk], rC, rtt), writes=(rtt,))
                    for (bk, fc, tt, rtt) in info:
                        kb.op("dve", lambda e, bk=bk, fc=fc, tt=tt, n=n: e.scalar_tensor_tensor(out=tt[:, :n], in0=PS[bk][:, 2:n + 2], scalar=colsb[:, l, C_CW2 + fc:C_CW2 + fc + 1], in1=tt[:, :n], op0=ALU.mult, op1=ALU.add),
                              reads=(rPS[bk], rC, rtt), writes=(rtt,))
                    g_ = cg[f % 2]; v_ = cv[f % 2]
                    kb.op("act", lambda e, g_=g_, n=n: e.activation(out=g_[:, :n], in_=g_[:, :n], func=AF.Gelu_apprx_tanh), reads=(rcg[f % 2],), writes=(rcg[f % 2],))
                    kb.op("pool", lambda e, g_=g_, v_=v_, f=f, n=n: e.tensor_tensor(out=GT[:, f, :n], in0=g_[:, :n], in1=v_[:, :n], op=ALU.mult), reads=(rcg[f % 2], rcv[f % 2]), writes=(rGT,))
                for mo in range(8):
                    bk = 4 + (mo % 2)
                    for k in range(22):
                        kb.op("pe", lambda e, k=k, mo=mo, bk=bk, n=n: e.matmul(PS[bk][:, :n], lhsT=WD[:, k, mo * 128:(mo + 1) * 128], rhs=GT[:, k, :n], start=(k == 0), stop=(k == 21)),
                              reads=(rWD, rGT), writes=(rPS[bk],), inc=(k == 21))
                    kb.op("dve", lambda e, mo=mo, bk=bk, n=n, i=i: e.scalar_tensor_tensor(out=R[:, mo, :n], in0=X1[i][:, mo, :n], scalar=ALPHA, in1=PS[bk][:, :n], op0=ALU.mult, op1=ALU.add),
                          reads=(rX1[i], rPS[bk]), writes=(rR,))
                layer_norm(lt, R, rR, n, C_LN2G, C_LN2B, l, Yf, rY, None, None, 6, 7, "ln2")
                if not final:
                    kb.dma("pool", J.XT[:, :, t0:t0 + n].rearrange("c p n -> p c n"), Yf[:, :, :n], rY, J.rXT, waw=False)
                else:
                    for s0 in range(0, n, 128):
                        sn = min(128, n - s0)
                        ot = OT[oti[0] % 2]; rot = rOT[oti[0] % 2]; oti[0] += 1
                        for hc in range(2):
                            bk = hc
                            for cc in range(4):
                                c = hc * 4 + cc
                                kb.op("pe", lambda e, c=c, cc=cc, bk=bk, s0=s0, sn=sn: e.transpose(PS[bk][:sn, cc * 128:(cc + 1) * 128], Yf[:, c, s0:s0 + sn], identf[:]),
                                      reads=(rY, rC), writes=(rPS[bk],), inc=(cc == 3))
                            kb.op("act", lambda e, hc=hc, bk=bk, ot=ot, sn=sn: e.copy(out=ot[:sn, hc * 512:(hc + 1) * 512], in_=PS[bk][:sn, :]), reads=(rPS[bk],), writes=(rot,))
                        kb.dma("pool", J.y[t0 + s0:t0 + s0 + sn, :], ot[:sn, :], rot, J.rXT, waw=False)
            kb.barrier()

    import os
    stop = int(os.environ.get("KSTOP", "999"))
    plist = []
    for l in range(DEPTH):
        plist.append(lambda l=l: phase_a(S, l))
        plist.append(lambda l=l: all_gather(SHL, S.SHA[3:3 + S.TCH], rSHL, S.rSHA))
        plist.append(lambda l=l: phase_a(P, l))
        for J in jobs:
            plist.append(lambda l=l, J=J: phase_natten(J, l))
            plist.append(lambda l=l, J=J: phase_fourier(J, l))
            plist.append(lambda l=l, J=J: phase_mla(J, l))
            plist.append(lambda l=l, J=J: phase_wout(J, l))
        plist.append(lambda l=l: all_gather(HL, HGP[1:17], rHL, rHGP))
        for J in jobs:
            plist.append(lambda l=l, J=J: phase_ffn(J, l, l == DEPTH - 1))
    for i, f in enumerate(plist):
        if i >= stop:
            break
        f()
    kb.barrier()
    gs.close()
    return nc, kb


def _cols_layout(v):
    return np.ascontiguousarray(v.reshape(-1, 128).T)


def _bias_tables(rpb, T, tiles):
    rows = T // 64
    kh = min(8, rows)
    out = np.full((5, 7, 128, 4, 128), -30000.0, np.float32)
    a = np.arange(128)
    ra = a // 64; ca = a % 64
    cstart = np.clip(ca - 8, 0, 64 - 16)
    for s, gi in enumerate(tiles):
        rq = 2 * gi + ra
        rstart = np.clip(rq - kh // 2, 0, rows - kh)
        for jj, j in enumerate(range(-3, 4)):
            rk = 2 * (gi + j) + ra
            K, Q = np.meshgrid(a, a, indexing="ij")
            rkK = rk[K]; ckK = ca[K]; rqQ = rq[Q]; cqQ = ca[Q]
            valid = (rkK >= 0) & (rkK < rows) & (rkK >= rstart[Q]) & (rkK < rstart[Q] + kh) & (ckK >= cstart[Q]) & (ckK < cstart[Q] + 16)
            ro = np.clip(rkK - rqQ + 7, 0, 14); co = np.clip(ckK - cqQ + 15, 0, 30)
            for h in range(4):
                vals = rpb[h][ro, co]
                out[s, jj, :, h, :] = np.where(valid, vals, np.float32(-30000.0))
    return out.reshape(35, 128, 512)


def _rope_tab(pos):
    inv = (10000.0 ** (-np.arange(0, 32, 2, dtype=np.float32) / np.float32(32))).astype(np.float32)
    ang = pos.astype(np.float32)[:, None] * inv[None, :]
    c = np.cos(ang).astype(np.float32).T
    s = np.sin(ang).astype(np.float32).T
    return np.stack([np.concatenate([c, c], 0), np.concatenate([-s, s], 0)], 0).astype(np.float32)


def _fft_tabs(T, L, q0):
    N1 = T // 128
    NK2 = L // N1
    t1 = np.arange(N1)[:, None]; k1 = np.arange(N1)[None, :]
    al = 2 * np.pi * ((t1 * k1) % N1) / N1
    f1 = np.stack([np.concatenate([np.cos(al), -np.sin(al)], 1), np.concatenate([-np.sin(al), -np.cos(al)], 1)], 0)
    t2 = np.arange(128)[:, None, None]
    kk1 = np.arange(N1)[None, :, None]
    k2 = (q0 // N1 + np.arange(NK2))[None, None, :]
    tp = kk1 + N1 * k2
    th = 2 * np.pi * ((t2 * tp) % T) / T
    g3 = np.stack([np.cos(th), np.sin(th)], 0).reshape(2, 128, N1 * NK2)
    return f1.astype(NPBF), g3.astype(NPBF)


def _cs_tab():
    c = np.arange(64)[:, None]; cp = np.arange(64)[None, :]
    ph = 2 * np.pi * ((c * cp) % 64) / 64
    t = np.zeros((256, 4, 2, 64), np.float64)
    for g in range(4):
        t[g * 64:(g + 1) * 64, g, 0, :] = np.cos(ph)
        t[g * 64:(g + 1) * 64, g, 1, :] = np.sin(ph)
    t = t.reshape(2, 128, 512).transpose(1, 0, 2)
    return np.ascontiguousarray(t).astype(NPBF)


def prep_inputs(inp, TP, TS):
    f = lambda k: np.asarray(inp[k], np.float32)
    LS = TS // NCORE
    w_in = f("w_in")
    o = np.cumsum([0, 256, 256, 256, 256, 256, 128, 32, 512])
    qa, ka, va, fb, cq, ckv, kr, sg = [np.arange(o[i], o[i + 1]) for i in range(8)]
    kr_sw = np.concatenate([kr[16:], kr[:16]])
    perm = np.concatenate([qa, ka, fb, cq, ckv, kr, kr_sw, sg[:256], va, sg[256:]])
    assert perm.size == ZC
    win = np.ascontiguousarray(w_in[:, :, perm])
    wuq0 = f("w_uq")
    rs = np.concatenate([np.arange(80, 96), np.arange(64, 80)])
    wuq = np.ascontiguousarray(np.concatenate([wuq0, wuq0[:, :, :, rs]], -1).reshape(DEPTH, 256, 512))
    wukv0 = f("w_ukv")
    wukv = np.ascontiguousarray(np.concatenate([wukv0[:, :, :, :64].reshape(DEPTH, 128, 256), wukv0[:, :, :, 64:].reshape(DEPTH, 128, 256)], -1))
    swt = np.ascontiguousarray(f("sgu_w").transpose(0, 3, 1, 2).reshape(DEPTH, 128, 512))
    cols = np.zeros((DEPTH, 128, NCOL), np.float32)
    for l in range(DEPTH):
        cols[l, :, C_LN1G:C_LN1G + 8] = _cols_layout(f("ln1_g")[l]); cols[l, :, C_LN1B:C_LN1B + 8] = _cols_layout(f("ln1_b")[l])
        cols[l, :, C_LN2G:C_LN2G + 8] = _cols_layout(f("ln2_g")[l]); cols[l, :, C_LN2B:C_LN2B + 8] = _cols_layout(f("ln2_b")[l])
        cols[l, :, C_QG:C_QG + 2] = _cols_layout(f("mla_q_g")[l]); cols[l, :, C_KVG:C_KVG + 1] = _cols_layout(f("mla_kv_g")[l])
        cw = f("conv_w")[l]
        cols[l, :, C_CW0:C_CW0 + 44] = _cols_layout(cw[0]); cols[l, :, C_CW1:C_CW1 + 44] = _cols_layout(cw[1]); cols[l, :, C_CW2:C_CW2 + 44] = _cols_layout(cw[2])
        cols[l, :, C_CB:C_CB + 44] = _cols_layout(f("conv_b")[l])
        cols[l, :, C_EG:C_EG + 8] = _cols_layout(f("emb_ln_g")); cols[l, :, C_EB:C_EB + 8] = _cols_layout(f("emb_ln_b"))
    sgb = np.zeros((DEPTH, 3, 512), np.float32)
    sgb[:, 0, :256] = f("sgu_ln_g"); sgb[:, 1, :256] = f("sgu_ln_b"); sgb[:, 2, :] = f("sgu_b").reshape(DEPTH, 512)
    rpb = f("na_rpb")
    shared = {"win": win, "wuq": wuq, "wukv": wukv, "swt": swt, "wout": f("w_out"), "wup": f("w_up"), "wdn": f("w_down"),
              "cols": cols, "sgb": sgb, "cst": _cs_tab(), "ident": np.eye(128, dtype=np.float32)}
    NCHp = TP // 128; NCHs = LS // 128
    btp = np.stack([_bias_tables(rpb[l], TP, [0, 1, 2, NCHp - 2, NCHp - 1]) for l in range(DEPTH)], 0)
    f1p, g3p = _fft_tabs(TP, TP, 0)
    ropep = _rope_tab(np.arange(TP))
    xp = f("x_prompt"); xs = f("x_sample")
    maps = []
    for c in range(NCORE):
        gb = c * NCHs
        bts = np.stack([_bias_tables(rpb[l], TS, [gb, gb + 1, gb + 2, gb + NCHs - 2, gb + NCHs - 1]) for l in range(DEPTH)], 0)
        f1s, g3s = _fft_tabs(TS, LS, c * LS)
        m = dict(shared)
        m.update({"x_p": np.ascontiguousarray(xp[c]), "x_s": np.ascontiguousarray(xs[0, c * LS:(c + 1) * LS]),
                  "bt_p": btp, "bt_s": bts, "rope_p": ropep, "rope_s": _rope_tab(np.arange(c * LS, (c + 1) * LS)),
                  "f1_p": f1p, "g3_p": g3p, "f1_s": f1s, "g3_s": g3s})
        maps.append(m)
    return maps


_CACHE = {}


def run(inp, TP, TS):
    key = (TP, TS)
    if key not in _CACHE:
        _CACHE[key] = build(TP, TS)[0]
    nc = _CACHE[key]
    maps = prep_inputs(inp, TP, TS)
    res = run_bass_kernel_spmd(nc, maps, core_ids=list(range(NCORE)))
    yp = np.stack([np.asarray(res.results[c]["y_p"], np.float32) for c in range(NCORE)], 0)
    ys = np.concatenate([np.asarray(res.results[c]["y_s"], np.float32) for c in range(NCORE)], 0)[None]
    return yp, ys


def kernel(**inputs):
    return run(inputs, 4096, 16384)
```
